# Optimizing a Trainium2 kernel written in Bass

```python
import math
import jax, jax.numpy as jnp
from jax import lax
import numpy as np

D_MODEL = 2048
BATCH = 2
SEQ = 8192
DEPTH = 4

CHUNK = 64
Q_BLOCK = 128
D_MIX = D_MODEL
D_ATT = D_MIX // 2
D_SSM = D_MIX - D_ATT
N_HEADS = 8
HEAD_DV = D_ATT // N_HEADS
HEAD_DK = HEAD_DV // 2
SSM_GROUP = 16
N_GROUPS = D_SSM // SSM_GROUP
STATE = 64
N_BUCKETS = 32
MAX_DISTANCE = 128
EPS = 1e-6
NEG_INF = -1e30
D_IN = 4 * D_ATT + 2 * D_SSM

kernel_name = "hybrid_diffattn_s5_parallel_heads"


def lambda_init_fn(layer_idx):
    return 0.8 - 0.6 * math.exp(-0.3 * layer_idx)


def rmsnorm(x, g):
    xf = x.astype(jnp.float32)
    y = xf * lax.rsqrt(jnp.mean(xf * xf, axis=-1, keepdims=True) + EPS)
    return (y * g.astype(jnp.float32)).astype(x.dtype)


def t5_bucket(rel):
    half = N_BUCKETS // 2
    max_exact = half // 2
    n = jnp.abs(rel)
    nf = jnp.maximum(n, 1).astype(jnp.float32)
    large = max_exact + (jnp.log(nf / max_exact) / math.log(MAX_DISTANCE / max_exact)
                         * (half - max_exact)).astype(jnp.int32)
    large = jnp.minimum(large, half - 1)
    return jnp.where(rel > 0, half, 0) + jnp.where(n < max_exact, n, large)


def diff_attention(q, k, v, lam, rel_bias):
    bsz, seq = q.shape[0], q.shape[1]
    scale = HEAD_DK ** -0.5
    k_pos = jnp.arange(seq, dtype=jnp.int32)
    k_chunk = k_pos // CHUNK
    table = rel_bias.astype(jnp.float32)

    def one_block(i):
        start = i * Q_BLOCK
        qb = lax.dynamic_slice_in_dim(q, start, Q_BLOCK, axis=1)
        q_pos = start + jnp.arange(Q_BLOCK, dtype=jnp.int32)
        logits = jnp.einsum("bqhmd,bkhmd->bmhqk", qb, k).astype(jnp.float32) * scale
        bias = jnp.transpose(table[t5_bucket(k_pos[None, :] - q_pos[:, None])], (2, 0, 1))
        allowed = k_chunk[None, :] <= (q_pos // CHUNK)[:, None]
        logits = jnp.where(allowed, logits + bias, NEG_INF)
        p = jax.nn.softmax(logits, axis=-1)
        w = p[:, 0] - lam * p[:, 1]
        return jnp.einsum("bhqk,bkhd->bqhd", w.astype(v.dtype), v)

    out = lax.map(one_block, jnp.arange(seq // Q_BLOCK, dtype=jnp.int32))
    return jnp.transpose(out, (1, 0, 2, 3, 4)).reshape(bsz, seq, N_HEADS, HEAD_DV)


def _diag_linear_combine(e1, e2):
    a1r, a1i, b1r, b1i = e1
    a2r, a2i, b2r, b2i = e2
    ar = a2r * a1r - a2i * a1i
    ai = a2r * a1i + a2i * a1r
    br = a2r * b1r - a2i * b1i + b2r
    bi = a2r * b1i + a2i * b1r + b2i
    return (ar, ai, br, bi)


def s5_ssm(u, a_re, a_im, log_dt, b_re, b_im, c_re, c_im, d_skip):
    f32 = jnp.float32
    bsz, seq = u.shape[0], u.shape[1]
    uf = u.astype(f32)
    ug = uf.reshape(bsz, seq, N_GROUPS, SSM_GROUP)
    a_re = a_re.astype(f32)
    a_im = a_im.astype(f32)
    b_re = b_re.astype(f32)
    b_im = b_im.astype(f32)
    dt = jnp.exp(log_dt.astype(f32))[:, None]
    mag = jnp.exp(dt * a_re)
    ab_re = mag * jnp.cos(dt * a_im)
    ab_im = mag * jnp.sin(dt * a_im)
    den = a_re * a_re + a_im * a_im
    nr = ab_re - 1.0
    cf_re = (nr * a_re + ab_im * a_im) / den
    cf_im = (ab_im * a_re - nr * a_im) / den
    bb_re = cf_re[..., None] * b_re - cf_im[..., None] * b_im
    bb_im = cf_re[..., None] * b_im + cf_im[..., None] * b_re
    bu_re = jnp.einsum("gpc,bsgc->bsgp", bb_re, ug)
    bu_im = jnp.einsum("gpc,bsgc->bsgp", bb_im, ug)
    at_re = jnp.broadcast_to(ab_re, (1, seq, N_GROUPS, STATE))
    at_im = jnp.broadcast_to(ab_im, (1, seq, N_GROUPS, STATE))
    _, _, h_re, h_im = lax.associative_scan(
        _diag_linear_combine, (at_re, at_im, bu_re, bu_im), axis=1)
    y = (jnp.einsum("gcp,bsgp->bsgc", c_re.astype(f32), h_re)
         - jnp.einsum("gcp,bsgp->bsgc", c_im.astype(f32), h_im))
    return y.reshape(bsz, seq, D_SSM) + d_skip.astype(f32) * uf


def setup_inputs(seed: int = 0) -> dict:
    key = jax.random.key(seed)
    ks = jax.random.split(key, 20)
    f32 = jnp.float32
    nrm = lambda k, shape, s: jax.random.normal(k, shape, f32) * s
    x = jax.random.normal(ks[0], (BATCH, SEQ, D_MODEL), f32)
    rel_bias = nrm(ks[1], (N_BUCKETS, N_HEADS), 0.1)
    pre_norm_g = 1.0 + nrm(ks[2], (DEPTH, D_MODEL), 0.02)
    post_norm_g = 1.0 + nrm(ks[3], (DEPTH, D_MODEL), 0.02)
    w_in = nrm(ks[4], (DEPTH, D_MODEL, D_IN), D_MODEL ** -0.5)
    lambda_q1 = nrm(ks[5], (DEPTH, HEAD_DK), 0.1)
    lambda_k1 = nrm(ks[6], (DEPTH, HEAD_DK), 0.1)
    lambda_q2 = nrm(ks[7], (DEPTH, HEAD_DK), 0.1)
    lambda_k2 = nrm(ks[8], (DEPTH, HEAD_DK), 0.1)
    subln_g = 1.0 + nrm(ks[9], (DEPTH, HEAD_DV), 0.02)
    ssm_a_re = -0.5 + nrm(ks[10], (DEPTH, N_GROUPS, STATE), 0.01)
    ssm_a_im = (math.pi * jnp.arange(STATE, dtype=f32))[None, None, :] + nrm(ks[11], (DEPTH, N_GROUPS, STATE), 0.01)
    ssm_log_dt = jax.random.uniform(ks[12], (DEPTH, N_GROUPS), f32, math.log(1e-3), math.log(1e-1))
    ssm_b_re = nrm(ks[13], (DEPTH, N_GROUPS, STATE, SSM_GROUP), (2 * SSM_GROUP) ** -0.5)
    ssm_b_im = nrm(ks[14], (DEPTH, N_GROUPS, STATE, SSM_GROUP), (2 * SSM_GROUP) ** -0.5)
    ssm_c_re = nrm(ks[15], (DEPTH, N_GROUPS, SSM_GROUP, STATE), (2 * STATE) ** -0.5)
    ssm_c_im = nrm(ks[16], (DEPTH, N_GROUPS, SSM_GROUP, STATE), (2 * STATE) ** -0.5)
    ssm_d = nrm(ks[17], (DEPTH, D_SSM), 1.0)
    w_glu = nrm(ks[18], (DEPTH, D_SSM, 2 * D_SSM), D_SSM ** -0.5)
    w_out = nrm(ks[19], (DEPTH, D_MIX, D_MODEL), D_MIX ** -0.5)
    return {"x": x, "rel_bias": rel_bias, "pre_norm_g": pre_norm_g, "post_norm_g": post_norm_g,
            "w_in": w_in, "lambda_q1": lambda_q1, "lambda_k1": lambda_k1,
            "lambda_q2": lambda_q2, "lambda_k2": lambda_k2, "subln_g": subln_g,
            "ssm_a_re": ssm_a_re, "ssm_a_im": ssm_a_im, "ssm_log_dt": ssm_log_dt,
            "ssm_b_re": ssm_b_re, "ssm_b_im": ssm_b_im, "ssm_c_re": ssm_c_re,
            "ssm_c_im": ssm_c_im, "ssm_d": ssm_d, "w_glu": w_glu, "w_out": w_out}


def reference(x, rel_bias, pre_norm_g, post_norm_g, w_in, lambda_q1, lambda_k1,
              lambda_q2, lambda_k2, subln_g, ssm_a_re, ssm_a_im, ssm_log_dt,
              ssm_b_re, ssm_b_im, ssm_c_re, ssm_c_im, ssm_d, w_glu, w_out):
    bsz, seq = x.shape[0], x.shape[1]
    split_at = [D_ATT, 2 * D_ATT, 3 * D_ATT, 4 * D_ATT, 4 * D_ATT + D_SSM]
    for l in range(DEPTH):
        h = rmsnorm(x, pre_norm_g[l])
        proj = jnp.einsum("bsd,de->bse", h, w_in[l])
        q, k, v, z_att, u, z_ssm = jnp.split(proj, split_at, axis=-1)

        lam_init = lambda_init_fn(l)
        lam = (jnp.exp(jnp.sum(lambda_q1[l].astype(jnp.float32) * lambda_k1[l].astype(jnp.float32)))
               - jnp.exp(jnp.sum(lambda_q2[l].astype(jnp.float32) * lambda_k2[l].astype(jnp.float32)))
               + lam_init)
        q = q.reshape(bsz, seq, N_HEADS, 2, HEAD_DK)
        k = k.reshape(bsz, seq, N_HEADS, 2, HEAD_DK)
        v = v.reshape(bsz, seq, N_HEADS, HEAD_DV)
        o_att = diff_attention(q, k, v, lam, rel_bias)
        o_att = rmsnorm(o_att, subln_g[l]) * (1.0 - lam_init)
        o_att = o_att.reshape(bsz, seq, D_ATT) * jax.nn.silu(z_att)

        y = s5_ssm(u, ssm_a_re[l], ssm_a_im[l], ssm_log_dt[l], ssm_b_re[l], ssm_b_im[l],
                   ssm_c_re[l], ssm_c_im[l], ssm_d[l]).astype(x.dtype)
        g = jnp.einsum("bsc,ce->bse", jax.nn.gelu(y), w_glu[l])
        g_val, g_gate = jnp.split(g, 2, axis=-1)
        o_ssm = g_val * jax.nn.sigmoid(g_gate) * jax.nn.silu(z_ssm)

        mix = jnp.einsum("bsc,cd->bsd", jnp.concatenate([o_att, o_ssm], axis=-1), w_out[l])
        x = x + rmsnorm(mix, post_norm_g[l])
    return x
```

```python
import math
import numpy as np
import ml_dtypes


from contextlib import ExitStack
import concourse.bass as bass
import concourse.mybir as mybir

F32 = mybir.dt.float32
BF16 = mybir.dt.bfloat16
AF = mybir.ActivationFunctionType
ALU = mybir.AluOpType


class Sem:
    def __init__(self, h, name, is_dma):
        self.h = h
        self.name = name
        self.n = 0
        self.is_dma = is_dma


class Buf:
    def __init__(self, name, t=None, dsem=None):
        self.name = name
        self.t = t
        self.last_w = None
        self.readers = {}
        self.dsem = dsem

    def __getitem__(self, k):
        return self.t[k]


class Ctx:
    def __init__(self, nc):
        self.nc = nc
        self.es = ExitStack()
        self.engs = {"pe": nc.tensor, "act": nc.scalar, "dve": nc.vector, "pool": nc.gpsimd, "sp": nc.sync}
        self.prog = {}
        for k in ("pe", "act", "dve", "pool"):
            self.prog[k] = self.sem("prog_" + k, False)
        self.known = {k: {} for k in self.engs}
        self.nsem = 4
        self.ninstr = {k: 0 for k in self.engs}

    def sem(self, name, is_dma=True):
        h = self.es.enter_context(self.nc.semaphore(name))
        return Sem(h, name, is_dma)

    def sb(self, name, shape, dt, dsem=None, scope=None):
        t = (scope or self.es).enter_context(self.nc.sbuf_tensor(name, list(shape), dt))
        return Buf(name, t, dsem)

    def ps(self, name, shape, dt, scope=None):
        t = (scope or self.es).enter_context(self.nc.psum_tensor(name, list(shape), dt))
        return Buf(name, t)

    def view(self, name, dsem=None):
        return Buf(name, None, dsem)

    def _wait(self, ek, ev):
        sem, val = ev
        if sem.is_dma:
            val = sem.n
        kn = self.known[ek]
        if kn.get(sem.name, 0) >= val:
            return
        self.engs[ek].wait_ge(sem.h, val)
        self.ninstr[ek] += 1
        kn[sem.name] = val

    def _deps(self, ek, reads, writes):
        evs = []
        for b in reads:
            if b.last_w is not None:
                evs.append(b.last_w)
        for b in writes:
            if b.last_w is not None:
                evs.append(b.last_w)
            evs.extend(b.readers.values())
        for ev in evs:
            if ek == "pe" and ev[0] is self.prog["pe"]:
                continue
            self._wait(ek, ev)

    def op(self, ek, fn, reads=(), writes=()):
        self._deps(ek, reads, writes)
        ins = fn(self.engs[ek])
        ps = self.prog[ek]
        ins.then_inc(ps.h, 1)
        ps.n += 1
        self.ninstr[ek] += 1
        ev = (ps, ps.n)
        for b in writes:
            b.last_w = ev
            b.readers = {}
        for b in reads:
            if b not in writes:
                b.readers[ek] = ev
        return ev

    def dma(self, qk, out_ap, in_ap, reads=(), writes=(), sem=None, **kw):
        if sem is None:
            for b in list(writes) + list(reads):
                if b.dsem is not None:
                    sem = b.dsem
                    break
        assert sem is not None, "dma needs a semaphore"
        self._deps(qk, reads, writes)
        ins = self.engs[qk].dma_start(out=out_ap, in_=in_ap, **kw)
        ins.then_inc(sem.h, 16)
        sem.n += 16
        self.ninstr[qk] += 1
        ev = (sem, sem.n)
        for b in writes:
            b.last_w = ev
            b.readers = {}
        for b in reads:
            if b not in writes:
                b.readers["dma:" + sem.name] = ev
        return ev

    def wait_all(self, ek, bufs):
        for b in bufs:
            if b.last_w is not None:
                self._wait(ek, b.last_w)
            for ev in b.readers.values():
                self._wait(ek, ev)

    def barrier(self, dma_sems=()):
        for ek in self.engs:
            for k2, ps in self.prog.items():
                if k2 == ek:
                    continue
                if ps.n > 0:
                    self._wait(ek, (ps, ps.n))
            for s in dma_sems:
                if s.n > 0:
                    self._wait(ek, (s, s.n))

    def close(self):
        self.es.close()


T = 2048
DM = 2048
NKC = 16
EPS = 1e-6


def phase1(c, x_in, w_in, gT_d, ident_d, outs, x_is_sbuf=None):
    nc = c.nc
    with ExitStack() as sc:
        s_c = c.sem("p1_const")
        gT = c.sb("p1_gT", [128, 16], F32, dsem=s_c, scope=sc)
        ident = c.sb("p1_ident", [128, 128], BF16, dsem=s_c, scope=sc)
        hT = c.sb("p1_hT", [128, NKC, T], BF16, scope=sc)
        hT_tiles = [c.view(f"p1_hT{i}") for i in range(16)]
        xt = [c.sb(f"p1_xt{i}", [128, DM], F32, dsem=c.sem(f"p1_xs{i}"), scope=sc) for i in range(2)]
        junk = c.sb("p1_junk", [128, DM], BF16, scope=sc)
        xs = [c.sb(f"p1_xsb{i}", [128, DM], BF16, scope=sc) for i in range(2)]
        ss = [c.sb(f"p1_ss{i}", [128, 1], F32, scope=sc) for i in range(2)]
        rstd = [c.sb(f"p1_rstd{i}", [128, 1], F32, scope=sc) for i in range(2)]
        pT = [c.ps(f"p1_pT{i}", [128, DM], BF16, scope=sc) for i in range(2)]
        acc = [c.ps(f"p1_acc{i}", [128, 512], F32, scope=sc) for i in range(3)]
        wb = [c.sb(f"p1_wb{i}", [128, NKC, 512], BF16, dsem=c.sem(f"p1_ws{i}"), scope=sc) for i in range(2)]
        ost = [c.sb(f"p1_ost{i}", [128, T], BF16, dsem=c.sem(f"p1_os{i}"), scope=sc) for i in range(2)]
        vst = [c.sb(f"p1_vst{i}", [128, 512], BF16, dsem=c.sem(f"p1_vs{i}"), scope=sc) for i in range(2)]

        c.dma("sp", gT[:, :], gT_d, writes=[gT])
        c.dma("sp", ident[:, :], ident_d, writes=[ident])

        def load_w(cb):
            b = wb[cb % 2]
            src = w_in[:, cb * 512:(cb + 1) * 512].rearrange("(k p) c -> p k c", p=128)
            for h in range(2):
                c.dma("pool", b[:, h * 8:(h + 1) * 8, :], src[:, h * 8:(h + 1) * 8, :], writes=[b])
        load_w(0)

        for i in range(16):
            xb = xt[i % 2]
            c.dma("sp", xb[:, :], x_in[i * 128:(i + 1) * 128, :], writes=[xb])
            s, r, xsb, pt = ss[i % 2], rstd[i % 2], xs[i % 2], pT[i % 2]
            c.op("act", lambda e: e.activation(out=junk[:, :], in_=xb[:, :], func=AF.Square, accum_out=s[:, :]),
                 reads=[xb], writes=[junk, s])
            c.op("dve", lambda e: e.tensor_scalar(out=r[:, :], in0=s[:, :], scalar1=1.0 / DM, scalar2=EPS,
                                                  op0=ALU.mult, op1=ALU.add), reads=[s], writes=[r])
            c.op("act", lambda e: e.activation(out=r[:, :], in_=r[:, :], func=AF.Sqrt), reads=[r], writes=[r])
            c.op("dve", lambda e: e.reciprocal(out=r[:, :], in_=r[:, :]), reads=[r], writes=[r])
            c.op("act", lambda e: e.activation(out=xsb[:, :], in_=xb[:, :], func=AF.Identity, scale=r[:, 0:1]),
                 reads=[xb, r], writes=[xsb])
            for kc in range(NKC):
                c.op("pe", lambda e: e.transpose(out=pt[:, kc * 128:(kc + 1) * 128],
                                                 in_=xsb[:, kc * 128:(kc + 1) * 128], identity=ident[:, :]),
                     reads=[xsb, ident], writes=[pt])
            c.op("dve", lambda e: e.tensor_tensor(
                out=hT[:, :, i * 128:(i + 1) * 128],
                in0=pt[:, :].rearrange("p (k t) -> p k t", k=NKC),
                in1=gT[:, :].unsqueeze(2).to_broadcast([128, NKC, 128]), op=ALU.mult),
                reads=[pt, gT], writes=[hT_tiles[i]])

        kinds = ["q", "q", "k", "k", "v", "v", "za", "za", "u", "u", "zs", "zs"]
        dst = {"q": outs["qT"], "k": outs["kT"], "za": outs["szaT"], "u": outs["uT"], "zs": outs["szsT"]}
        nacc = 0
        nost = 0
        nvst = 0
        for cb in range(12):
            if cb + 1 < 12:
                load_w(cb + 1)
            b = wb[cb % 2]
            kind = kinds[cb]
            if kind == "v":
                for i in range(16):
                    a = acc[nacc % 3]; nacc += 1
                    for kc in range(NKC):
                        c.op("pe", lambda e: e.matmul(a[:, :], lhsT=hT[:, kc, i * 128:(i + 1) * 128],
                                                      rhs=b[:, kc, :], start=(kc == 0), stop=(kc == NKC - 1)),
                             reads=[hT_tiles[i], b], writes=[a])
                    st = vst[nvst % 2]; nvst += 1
                    c.op("act", lambda e: e.activation(out=st[:, :], in_=a[:, :], func=AF.Copy),
                         reads=[a], writes=[st])
                    col0 = (cb - 4) * 512
                    c.dma("sp", outs["v"][i * 128:(i + 1) * 128, col0:col0 + 512], st[:, :], reads=[st])
            else:
                for s4 in range(4):
                    st = ost[nost % 2]; nost += 1
                    for tg in range(4):
                        a = acc[nacc % 3]; nacc += 1
                        for kc in range(NKC):
                            c.op("pe", lambda e: e.matmul(a[:, :], lhsT=b[:, kc, s4 * 128:(s4 + 1) * 128],
                                                          rhs=hT[:, kc, tg * 512:(tg + 1) * 512],
                                                          start=(kc == 0), stop=(kc == NKC - 1)),
                                 reads=hT_tiles[tg * 4:(tg + 1) * 4] + [b], writes=[a])
                        o = st[:, tg * 512:(tg + 1) * 512]
                        if kind == "q":
                            c.op("act", lambda e: e.activation(out=o, in_=a[:, :], func=AF.Copy, scale=0.125),
                                 reads=[a], writes=[st])
                        elif kind in ("k", "u"):
                            c.op("dve", lambda e: e.tensor_copy(out=o, in_=a[:, :]), reads=[a], writes=[st])
                        else:
                            c.op("act", lambda e: e.activation(out=o, in_=a[:, :], func=AF.Silu),
                                 reads=[a], writes=[st])
                    row0 = (cb % 2) * 512 + s4 * 128
                    c.dma("sp", dst[kind][row0:row0 + 128, :], st[:, :], reads=[st])
        c.wait_all("sp", ost + vst)
        c.barrier()


import math
import numpy as np

I32 = mybir.dt.int32
S = 8192
EPS = 1e-6
TWO_PI = 2 * math.pi
C1 = 6.28125
_c2 = np.array([TWO_PI - C1], dtype=np.float32)
C2 = float((_c2.view(np.uint32) & np.uint32(0xFFFFF000)).view(np.float32)[0])
C3 = TWO_PI - C1 - C2


def lam_init(l):
    return 0.8 - 0.6 * math.exp(-0.3 * l)


def sincos(c, ang, tmpi, tmpf, tmpm, out_sin, out_cos, shape):
    sl = (slice(None), slice(0, shape))
    c.op("dve", lambda e: e.tensor_scalar(out=tmpi[sl], in0=ang[sl], scalar1=1.0 / TWO_PI, scalar2=None, op0=ALU.mult),
         reads=[ang], writes=[tmpi])
    c.op("dve", lambda e: e.tensor_copy(out=tmpf[sl], in_=tmpi[sl]), reads=[tmpi], writes=[tmpf])
    for cc in (C1, C2, C3):
        c.op("dve", lambda e: e.scalar_tensor_tensor(out=ang[sl], in0=tmpf[sl], scalar=-cc, in1=ang[sl],
                                                     op0=ALU.mult, op1=ALU.add), reads=[tmpf, ang], writes=[ang])
    for shift, outb in ((0.0, out_sin), (math.pi / 2, out_cos)):
        c.op("dve", lambda e: e.tensor_scalar(out=outb[sl], in0=ang[sl], scalar1=shift, scalar2=None, op0=ALU.add),
             reads=[ang], writes=[outb])
        c.op("dve", lambda e: e.tensor_scalar(out=tmpm[sl], in0=outb[sl], scalar1=math.pi, scalar2=None, op0=ALU.is_gt),
             reads=[outb], writes=[tmpm])
        c.op("dve", lambda e: e.scalar_tensor_tensor(out=outb[sl], in0=tmpm[sl], scalar=-TWO_PI, in1=outb[sl],
                                                     op0=ALU.mult, op1=ALU.add), reads=[tmpm, outb], writes=[outb])
        c.op("dve", lambda e: e.tensor_scalar(out=tmpm[sl], in0=outb[sl], scalar1=-math.pi, scalar2=None, op0=ALU.is_lt),
             reads=[outb], writes=[tmpm])
        c.op("dve", lambda e: e.scalar_tensor_tensor(out=outb[sl], in0=tmpm[sl], scalar=TWO_PI, in1=outb[sl],
                                                     op0=ALU.mult, op1=ALU.add), reads=[tmpm, outb], writes=[outb])
        c.op("act", lambda e: e.activation(out=outb[sl], in_=outb[sl], func=AF.Sin), reads=[outb], writes=[outb])


def phase2_attn(c, qT_d, kT_d, v_d, cst, oaT_d):
    nc = c.nc
    with ExitStack() as sc:
        s_c = c.sem("a_const")
        kT = c.sb("a_kT", [128, 2, S], BF16, dsem=s_c, scope=sc)
        qT = c.sb("a_qT", [128, 2, S], BF16, dsem=s_c, scope=sc)
        v = c.sb("a_v", [128, 64, 256], BF16, dsem=s_c, scope=sc)
        biasD = c.sb("a_biasD", [128, 2, 2, 128], F32, dsem=s_c, scope=sc)
        maskD = c.sb("a_maskD", [128, 128], F32, dsem=s_c, scope=sc)
        c15 = c.sb("a_c15", [128, 2], F32, dsem=s_c, scope=sc)
        lam4 = c.sb("a_lam4", [128, 4, 64], F32, dsem=s_c, scope=sc)
        subg = c.sb("a_subg", [128, 1], F32, dsem=s_c, scope=sc)
        liv = c.sb("a_liv", [128, 2], F32, dsem=s_c, scope=sc)
        ident = c.sb("a_ident", [128, 128], BF16, dsem=s_c, scope=sc)
        ones = c.sb("a_ones", [128, 128], BF16, scope=sc)
        onesf = c.sb("a_onesf", [128, 128], F32, scope=sc)
        biasN = c.sb("a_biasN", [128, 2, 2, 128], BF16, scope=sc)
        lsum = c.sb("a_lsum", [128, 2], F32, scope=sc)
        ljunk = c.sb("a_ljunk", [128, 64], F32, scope=sc)
        neglam = c.sb("a_neglam", [128, 1], F32, scope=sc)
        gs = c.sb("a_gs", [128, 1], F32, scope=sc)
        pT = [c.sb(f"a_pT{i}", [128, 512], BF16, scope=sc) for i in range(3)]
        rs = c.sb("a_rs", [128, 512], F32, scope=sc)
        o1 = c.sb("a_o1", [128, 512], F32, scope=sc)
        o2 = c.sb("a_o2", [128, 512], F32, scope=sc)
        sq = c.sb("a_sq", [128, 512], F32, scope=sc)
        rstd = c.sb("a_rstd", [128, 512], F32, scope=sc)
        oa = [c.sb(f"a_oa{i}", [128, 512], BF16, dsem=c.sem(f"a_oas{i}"), scope=sc) for i in range(2)]
        sps = [c.ps(f"a_s{i}", [128, 512], F32, scope=sc) for i in range(3)]
        accO = [c.ps(f"a_accO{i}", [128, 512], F32, scope=sc) for i in range(2)]
        accS = [c.ps(f"a_accS{i}", [128, 512], F32, scope=sc) for i in range(2)]
        ssp = c.ps("a_ss", [128, 512], F32, scope=sc)

        for h in range(2):
            for half in range(2):
                sl = slice(half * 4096, (half + 1) * 4096)
                c.dma("sp", kT[:, h, sl], kT_d[h, :, sl], writes=[kT])
                c.dma("sp", qT[:, h, sl], qT_d[h, :, sl], writes=[qT])
        vsrc = v_d.rearrange("(t p) c -> p t c", p=128)
        for q4 in range(4):
            c.dma("sp", v[:, q4 * 16:(q4 + 1) * 16, :], vsrc[:, q4 * 16:(q4 + 1) * 16, :], writes=[v])
        c.dma("sp", biasD[:, :, :, :], cst["biasD"], writes=[biasD])
        c.dma("sp", maskD[:, :], cst["maskD"], writes=[maskD])
        c.dma("sp", c15[:, :], cst["c15"], writes=[c15])
        c.dma("sp", lam4[:, :, :], cst["lam4"], writes=[lam4])
        c.dma("sp", subg[:, :], cst["subg"], writes=[subg])
        c.dma("sp", liv[:, :], cst["liv"], writes=[liv])
        c.dma("sp", ident[:, :], cst["ident"], writes=[ident])

        c.op("dve", lambda e: e.memset(ones[:, :], 1.0), writes=[ones])
        c.op("dve", lambda e: e.memset(onesf[:, :], 1.0), writes=[onesf])
        for h in range(2):
            for dd in range(2):
                c.op("dve", lambda e: e.tensor_scalar(out=biasD[:, h, dd, :], in0=biasD[:, h, dd, :],
                                                      scalar1=c15[:, h:h + 1], scalar2=None, op0=ALU.subtract),
                     reads=[biasD, c15], writes=[biasD])
            c.op("dve", lambda e: e.tensor_tensor(out=biasD[:, h, 0, :], in0=biasD[:, h, 0, :], in1=maskD[:, :], op=ALU.add),
                 reads=[biasD, maskD], writes=[biasD])
        c.op("dve", lambda e: e.tensor_copy(out=biasN[:, :, :, :], in_=biasD[:, :, :, :]), reads=[biasD], writes=[biasN])
        for j in range(2):
            c.op("dve", lambda e: e.tensor_tensor(out=ljunk[:, :], in0=lam4[:, 2 * j, :], in1=lam4[:, 2 * j + 1, :], op=ALU.mult),
                 reads=[lam4], writes=[ljunk])
            c.op("dve", lambda e: e.reduce_sum(out=lsum[:, j:j + 1], in_=ljunk[:, :], axis=mybir.AxisListType.X),
                 reads=[ljunk], writes=[lsum])
        c.op("act", lambda e: e.activation(out=lsum[:, :], in_=lsum[:, :], func=AF.Exp), reads=[lsum], writes=[lsum])
        c.op("dve", lambda e: e.scalar_tensor_tensor(out=neglam[:, :], in0=lsum[:, 1:2], scalar=liv[:, 0:1], in1=lsum[:, 0:1],
                                                     op0=ALU.subtract, op1=ALU.subtract), reads=[lsum, liv], writes=[neglam])
        c.op("dve", lambda e: e.tensor_tensor(out=gs[:, :], in0=subg[:, :], in1=liv[:, 1:2], op=ALU.mult),
             reads=[subg, liv], writes=[gs])

        nS = 0
        nP = 0
        nOa = 0
        for h in range(2):
            for qt in range(16):
                nkb = 4 * qt + 4
                for m in range(2):
                    aO, aS = accO[m], accS[m]
                    mrows = slice(m * 64, (m + 1) * 64)
                    for kb in range(nkb):
                        r = kb - 4 * qt
                        q0 = max(r, 0) * 128
                        N = 512 - q0
                        sp_ = sps[nS % 3]; nS += 1
                        adds = []
                        for qs in range(max(r, 0), 4):
                            dd = 4 * qt + qs - kb
                            if dd in (0, 1):
                                adds.append((qs, dd))
                        c.op("pe", lambda e: e.matmul(sp_[:, 0:N], lhsT=kT[mrows, h, kb * 128:(kb + 1) * 128],
                                                      rhs=qT[mrows, h, qt * 512 + q0:(qt + 1) * 512],
                                                      start=True, stop=(len(adds) == 0)),
                             reads=[kT, qT], writes=[sp_])
                        for ai, (qs, dd) in enumerate(adds):
                            c0 = qs * 128 - q0
                            c.op("pe", lambda e: e.matmul(sp_[:, c0:c0 + 128], lhsT=ident[:, :], rhs=biasN[:, h, dd, :],
                                                          start=False, stop=(ai == len(adds) - 1)),
                                 reads=[ident, biasN], writes=[sp_])
                        p_ = pT[nP % 3]; nP += 1
                        c.op("act", lambda e: e.activation(out=p_[:, 0:N], in_=sp_[:, 0:N], func=AF.Exp,
                                                           bias=c15[:, h:h + 1]),
                             reads=[sp_, c15], writes=[p_])
                        c.op("pe", lambda e: e.matmul(aO[:, q0:512], lhsT=v[:, kb, h * 128:(h + 1) * 128],
                                                      rhs=p_[:, 0:N], start=(kb == 0), stop=(kb == nkb - 1)),
                             reads=[v, p_], writes=[aO])
                        c.op("pe", lambda e: e.matmul(aS[:, q0:512], lhsT=ones[:, :], rhs=p_[:, 0:N],
                                                      start=(kb == 0), stop=(kb == nkb - 1)),
                             reads=[ones, p_], writes=[aS])
                    om = o1 if m == 0 else o2
                    c.op("dve", lambda e: e.reciprocal(out=rs[:, :], in_=aS[:, :]), reads=[aS], writes=[rs])
                    c.op("dve", lambda e: e.tensor_tensor(out=om[:, :], in0=aO[:, :], in1=rs[:, :], op=ALU.mult),
                         reads=[aO, rs], writes=[om])
                c.op("dve", lambda e: e.scalar_tensor_tensor(out=o1[:, :], in0=o2[:, :], scalar=neglam[:, 0:1], in1=o1[:, :],
                                                             op0=ALU.mult, op1=ALU.add), reads=[o2, neglam, o1], writes=[o1])
                c.op("pool", lambda e: e.tensor_tensor(out=sq[:, :], in0=o1[:, :], in1=o1[:, :], op=ALU.mult),
                     reads=[o1], writes=[sq])
                c.op("pe", lambda e: e.matmul(ssp[:, :], lhsT=onesf[:, :], rhs=sq[:, :], start=True, stop=True),
                     reads=[onesf, sq], writes=[ssp])
                c.op("dve", lambda e: e.tensor_scalar(out=rstd[:, :], in0=ssp[:, :], scalar1=1.0 / 128, scalar2=EPS,
                                                      op0=ALU.mult, op1=ALU.add), reads=[ssp], writes=[rstd])
                c.op("act", lambda e: e.activation(out=rstd[:, :], in_=rstd[:, :], func=AF.Sqrt), reads=[rstd], writes=[rstd])
                c.op("dve", lambda e: e.reciprocal(out=rstd[:, :], in_=rstd[:, :]), reads=[rstd], writes=[rstd])
                ob = oa[nOa % 2]; nOa += 1
                c.op("dve", lambda e: e.scalar_tensor_tensor(out=ob[:, :], in0=o1[:, :], scalar=gs[:, 0:1], in1=rstd[:, :],
                                                             op0=ALU.mult, op1=ALU.mult), reads=[o1, gs, rstd], writes=[ob])
                c.dma("sp", oaT_d[h * 128:(h + 1) * 128, qt * 512:(qt + 1) * 512], ob[:, :], reads=[ob])
        c.wait_all("sp", oa)
        c.barrier()


def phase2_ssm(c, uT_d, cst, gyT_d):
    nc = c.nc
    NT = 513
    with ExitStack() as sc:
        s_c = c.sem("s_const")
        uT = c.sb("s_uT", [128, 2, S], BF16, dsem=s_c, scope=sc)
        acol = c.sb("s_acol", [128, 3, 8], F32, dsem=s_c, scope=sc)
        arow = c.sb("s_arow", [128, 3, 1024], F32, dsem=s_c, scope=sc)
        BTr = c.sb("s_BTr", [128, 1024], F32, dsem=s_c, scope=sc)
        BTi = c.sb("s_BTi", [128, 1024], F32, dsem=s_c, scope=sc)
        CTr = c.sb("s_CTr", [128, 1024], F32, dsem=s_c, scope=sc)
        CTi = c.sb("s_CTi", [128, 1024], F32, dsem=s_c, scope=sc)
        dcol = c.sb("s_dcol", [128, 2], F32, dsem=s_c, scope=sc)
        tau = c.sb("s_tau", [128, 520], F32, dsem=s_c, scope=sc)
        for h in range(2):
            for half in range(2):
                sl = slice(half * 4096, (half + 1) * 4096)
                c.dma("sp", uT[:, h, sl], uT_d[h * 128:(h + 1) * 128, sl], writes=[uT])
        c.dma("sp", acol[:, :, :], cst["acol"], writes=[acol])
        c.dma("sp", arow[:, :, :], cst["arow"], writes=[arow])
        c.dma("sp", BTr[:, :], cst["BTr"].rearrange("p g c -> p (g c)"), writes=[BTr])
        c.dma("sp", BTi[:, :], cst["BTi"].rearrange("p g c -> p (g c)"), writes=[BTi])
        c.dma("sp", CTr[:, :], cst["CTr"].rearrange("p g c -> p (g c)"), writes=[CTr])
        c.dma("sp", CTi[:, :], cst["CTi"].rearrange("p g c -> p (g c)"), writes=[CTi])
        c.dma("sp", dcol[:, :], cst["dcol"], writes=[dcol])
        c.dma("sp", tau[:, :], cst["tau"], writes=[tau])

        BT_re = c.sb("s_BT_re", [128, 1024], BF16, scope=sc)
        BT_im = c.sb("s_BT_im", [128, 1024], BF16, scope=sc)
        CT_re = c.sb("s_CT_re", [128, 1024], BF16, scope=sc)
        CT_imn = c.sb("s_CT_imn", [128, 1024], BF16, scope=sc)
        rcol = c.sb("s_rcol", [128, 8], F32, scope=sc)
        thc = c.sb("s_thc", [128, 8], F32, scope=sc)
        dtc = c.sb("s_dtc", [128, 8], F32, scope=sc)
        cr = [c.sb(f"s_cr{g}", [128, 520], F32, scope=sc) for g in range(8)]
        si = [c.sb(f"s_si{g}", [128, 520], F32, scope=sc) for g in range(8)]
        init_re = [c.sb(f"s_inr{g}", [128, 1], F32, scope=sc) for g in range(8)]
        init_im = [c.sb(f"s_ini{g}", [128, 1], F32, scope=sc) for g in range(8)]

        with ExitStack() as st:
            ang = c.sb("s_ang", [128, 1024], F32, scope=st)
            tmpi = c.sb("s_tmpi", [128, 1024], I32, scope=st)
            tmpf = c.sb("s_tmpf", [128, 1024], F32, scope=st)
            tmpm = c.sb("s_tmpm", [128, 1024], F32, scope=st)
            sn = c.sb("s_sn", [128, 1024], F32, scope=st)
            cs = c.sb("s_cs", [128, 1024], F32, scope=st)
            dtr = c.sb("s_dtr", [128, 1024], F32, scope=st)
            mag = c.sb("s_mag", [128, 1024], F32, scope=st)
            den = c.sb("s_den", [128, 1024], F32, scope=st)
            t1 = c.sb("s_t1", [128, 1024], F32, scope=st)
            t2 = c.sb("s_t2", [128, 1024], F32, scope=st)
            cfr = c.sb("s_cfr", [128, 1024], F32, scope=st)
            cfi = c.sb("s_cfi", [128, 1024], F32, scope=st)
            A = slice(None)
            c.op("act", lambda e: e.activation(out=dtc[:, :], in_=acol[:, 2, :], func=AF.Exp), reads=[acol], writes=[dtc])
            c.op("dve", lambda e: e.tensor_tensor(out=rcol[:, :], in0=dtc[:, :], in1=acol[:, 0, :], op=ALU.mult),
                 reads=[dtc, acol], writes=[rcol])
            c.op("act", lambda e: e.activation(out=rcol[:, :], in_=rcol[:, :], func=AF.Exp), reads=[rcol], writes=[rcol])
            c.op("dve", lambda e: e.tensor_tensor(out=thc[:, :], in0=dtc[:, :], in1=acol[:, 1, :], op=ALU.mult),
                 reads=[dtc, acol], writes=[thc])
            for g in range(8):
                c.op("dve", lambda e: e.tensor_scalar(out=ang[:, 0:NT], in0=tau[:, 0:NT], scalar1=thc[:, g:g + 1], scalar2=None,
                                                      op0=ALU.mult), reads=[tau, thc], writes=[ang])
                sincos(c, ang, tmpi, tmpf, tmpm, si[g], cr[g], NT)
            c.op("act", lambda e: e.activation(out=dtr[:, :], in_=arow[:, 2, :], func=AF.Exp), reads=[arow], writes=[dtr])
            c.op("dve", lambda e: e.tensor_tensor(out=mag[:, :], in0=dtr[:, :], in1=arow[:, 0, :], op=ALU.mult),
                 reads=[dtr, arow], writes=[mag])
            c.op("act", lambda e: e.activation(out=mag[:, :], in_=mag[:, :], func=AF.Exp), reads=[mag], writes=[mag])
            c.op("dve", lambda e: e.tensor_tensor(out=ang[:, :], in0=dtr[:, :], in1=arow[:, 1, :], op=ALU.mult),
                 reads=[dtr, arow], writes=[ang])
            sincos(c, ang, tmpi, tmpf, tmpm, sn, cs, 1024)
            c.op("dve", lambda e: e.tensor_tensor(out=cs[:, :], in0=cs[:, :], in1=mag[:, :], op=ALU.mult), reads=[cs, mag], writes=[cs])
            c.op("dve", lambda e: e.tensor_tensor(out=sn[:, :], in0=sn[:, :], in1=mag[:, :], op=ALU.mult), reads=[sn, mag], writes=[sn])
            c.op("dve", lambda e: e.tensor_scalar(out=cs[:, :], in0=cs[:, :], scalar1=-1.0, scalar2=None, op0=ALU.add), reads=[cs], writes=[cs])
            c.op("dve", lambda e: e.tensor_tensor(out=den[:, :], in0=arow[:, 0, :], in1=arow[:, 0, :], op=ALU.mult), reads=[arow], writes=[den])
            c.op("dve", lambda e: e.tensor_tensor(out=t1[:, :], in0=arow[:, 1, :], in1=arow[:, 1, :], op=ALU.mult), reads=[arow], writes=[t1])
            c.op("dve", lambda e: e.tensor_tensor(out=den[:, :], in0=den[:, :], in1=t1[:, :], op=ALU.add), reads=[den, t1], writes=[den])
            c.op("dve", lambda e: e.reciprocal(out=den[:, :], in_=den[:, :]), reads=[den], writes=[den])
            c.op("dve", lambda e: e.tensor_tensor(out=t1[:, :], in0=cs[:, :], in1=arow[:, 0, :], op=ALU.mult), reads=[cs, arow], writes=[t1])
            c.op("dve", lambda e: e.tensor_tensor(out=t2[:, :], in0=sn[:, :], in1=arow[:, 1, :], op=ALU.mult), reads=[sn, arow], writes=[t2])
            c.op("dve", lambda e: e.tensor_tensor(out=t1[:, :], in0=t1[:, :], in1=t2[:, :], op=ALU.add), reads=[t1, t2], writes=[t1])
            c.op("dve", lambda e: e.tensor_tensor(out=cfr[:, :], in0=t1[:, :], in1=den[:, :], op=ALU.mult), reads=[t1, den], writes=[cfr])
            c.op("dve", lambda e: e.tensor_tensor(out=t1[:, :], in0=sn[:, :], in1=arow[:, 0, :], op=ALU.mult), reads=[sn, arow], writes=[t1])
            c.op("dve", lambda e: e.tensor_tensor(out=t2[:, :], in0=cs[:, :], in1=arow[:, 1, :], op=ALU.mult), reads=[cs, arow], writes=[t2])
            c.op("dve", lambda e: e.tensor_tensor(out=t1[:, :], in0=t1[:, :], in1=t2[:, :], op=ALU.subtract), reads=[t1, t2], writes=[t1])
            c.op("dve", lambda e: e.tensor_tensor(out=cfi[:, :], in0=t1[:, :], in1=den[:, :], op=ALU.mult), reads=[t1, den], writes=[cfi])
            c.op("dve", lambda e: e.tensor_tensor(out=t1[:, :], in0=cfr[:, :], in1=BTr[:, :], op=ALU.mult), reads=[cfr, BTr], writes=[t1])
            c.op("dve", lambda e: e.tensor_tensor(out=t2[:, :], in0=cfi[:, :], in1=BTi[:, :], op=ALU.mult), reads=[cfi, BTi], writes=[t2])
            c.op("dve", lambda e: e.tensor_tensor(out=BT_re[:, :], in0=t1[:, :], in1=t2[:, :], op=ALU.subtract), reads=[t1, t2], writes=[BT_re])
            c.op("dve", lambda e: e.tensor_tensor(out=t1[:, :], in0=cfr[:, :], in1=BTi[:, :], op=ALU.mult), reads=[cfr, BTi], writes=[t1])
            c.op("dve", lambda e: e.tensor_tensor(out=t2[:, :], in0=cfi[:, :], in1=BTr[:, :], op=ALU.mult), reads=[cfi, BTr], writes=[t2])
            c.op("dve", lambda e: e.tensor_tensor(out=BT_im[:, :], in0=t1[:, :], in1=t2[:, :], op=ALU.add), reads=[t1, t2], writes=[BT_im])
            c.op("dve", lambda e: e.tensor_copy(out=CT_re[:, :], in_=CTr[:, :]), reads=[CTr], writes=[CT_re])
            c.op("dve", lambda e: e.tensor_scalar(out=CT_imn[:, :], in0=CTi[:, :], scalar1=-1.0, scalar2=None, op0=ALU.mult),
                 reads=[CTi], writes=[CT_imn])
            for g in range(8):
                c.op("dve", lambda e: e.memset(init_re[g][:, :], 0.0), writes=[init_re[g]])
                c.op("dve", lambda e: e.memset(init_im[g][:, :], 0.0), writes=[init_im[g]])
            c.barrier()

        bur = [c.ps(f"s_bur{i}", [128, 512], F32, scope=sc) for i in range(2)]
        bui = [c.ps(f"s_bui{i}", [128, 512], F32, scope=sc) for i in range(2)]
        yps = [c.ps(f"s_y{i}", [128, 512], F32, scope=sc) for i in range(2)]
        NB = 2
        t = [[c.sb(f"s_t{j}_{i}", [128, 512], F32, scope=sc) for j in range(4)] for i in range(NB)]
        w_re = [c.sb(f"s_wre{i}", [128, 512], F32, scope=sc) for i in range(NB)]
        w_im = [c.sb(f"s_wim{i}", [128, 512], F32, scope=sc) for i in range(NB)]
        g_re = [c.sb(f"s_gre{i}", [128, 512], F32, scope=sc) for i in range(NB)]
        g_im = [c.sb(f"s_gim{i}", [128, 512], F32, scope=sc) for i in range(NB)]
        uu = [[c.sb(f"s_u{j}_{i}", [128, 512], F32, scope=sc) for j in range(4)] for i in range(NB)]
        h_re = [c.sb(f"s_hre{i}", [128, 512], BF16, scope=sc) for i in range(NB)]
        h_im = [c.sb(f"s_him{i}", [128, 512], BF16, scope=sc) for i in range(NB)]
        tmp1 = c.sb("s_tmp1", [128, 1], F32, scope=sc)
        ysb = [c.sb(f"s_ysb{i}", [128, 512], F32, scope=sc) for i in range(2)]
        gy = [c.sb(f"s_gy{i}", [128, 512], BF16, dsem=c.sem(f"s_gys{i}"), scope=sc) for i in range(2)]
        it = 0
        ny = 0
        for ch in range(16):
            csl = slice(ch * 512, (ch + 1) * 512)
            for g in range(8):
                tt, g4 = g // 4, g % 4
                b = it % NB; it += 1
                gsl = slice(g * 128, (g + 1) * 128)
                br_, bi_ = bur[b % 2], bui[b % 2]
                c.op("pe", lambda e: e.matmul(br_[:, :], lhsT=BT_re[:, gsl], rhs=uT[:, tt, csl], start=True, stop=True),
                     reads=[BT_re, uT], writes=[br_])
                c.op("pe", lambda e: e.matmul(bi_[:, :], lhsT=BT_im[:, gsl], rhs=uT[:, tt, csl], start=True, stop=True),
                     reads=[BT_im, uT], writes=[bi_])
                crg, sig = cr[g], si[g]
                T = t[b]
                c.op("dve", lambda e: e.tensor_tensor(out=T[0][:, :], in0=br_[:, :], in1=crg[:, 0:512], op=ALU.mult), reads=[br_, crg], writes=[T[0]])
                c.op("dve", lambda e: e.tensor_tensor(out=T[1][:, :], in0=bi_[:, :], in1=sig[:, 0:512], op=ALU.mult), reads=[bi_, sig], writes=[T[1]])
                c.op("dve", lambda e: e.tensor_tensor(out=T[2][:, :], in0=bi_[:, :], in1=crg[:, 0:512], op=ALU.mult), reads=[bi_, crg], writes=[T[2]])
                c.op("dve", lambda e: e.tensor_tensor(out=T[3][:, :], in0=br_[:, :], in1=sig[:, 0:512], op=ALU.mult), reads=[br_, sig], writes=[T[3]])
                c.op("pool", lambda e: e.tensor_tensor(out=w_re[b][:, :], in0=T[0][:, :], in1=T[1][:, :], op=ALU.add), reads=[T[0], T[1]], writes=[w_re[b]])
                c.op("pool", lambda e: e.tensor_tensor(out=w_im[b][:, :], in0=T[2][:, :], in1=T[3][:, :], op=ALU.subtract), reads=[T[2], T[3]], writes=[w_im[b]])
                rbc = rcol[:, g:g + 1].to_broadcast([128, 512])
                c.op("dve", lambda e: e.tensor_tensor_scan(out=g_re[b][:, :], data0=rbc, data1=w_re[b][:, :], initial=init_re[g][:, 0:1],
                                                           op0=ALU.mult, op1=ALU.add), reads=[rcol, w_re[b], init_re[g]], writes=[g_re[b]])
                c.op("dve", lambda e: e.tensor_tensor_scan(out=g_im[b][:, :], data0=rbc, data1=w_im[b][:, :], initial=init_im[g][:, 0:1],
                                                           op0=ALU.mult, op1=ALU.add), reads=[rcol, w_im[b], init_im[g]], writes=[g_im[b]])
                gl_r, gl_i = g_re[b][:, 511:512], g_im[b][:, 511:512]
                c512, s512 = crg[:, 512:513], sig[:, 512:513]
                c.op("dve", lambda e: e.tensor_tensor(out=tmp1[:, :], in0=gl_i, in1=s512, op=ALU.mult), reads=[g_im[b], sig], writes=[tmp1])
                c.op("dve", lambda e: e.scalar_tensor_tensor(out=init_re[g][:, :], in0=gl_r, scalar=c512, in1=tmp1[:, :],
                                                             op0=ALU.mult, op1=ALU.subtract), reads=[g_re[b], crg, tmp1], writes=[init_re[g]])
                c.op("dve", lambda e: e.tensor_tensor(out=tmp1[:, :], in0=gl_i, in1=c512, op=ALU.mult), reads=[g_im[b], crg], writes=[tmp1])
                c.op("dve", lambda e: e.scalar_tensor_tensor(out=init_im[g][:, :], in0=gl_r, scalar=s512, in1=tmp1[:, :],
                                                             op0=ALU.mult, op1=ALU.add), reads=[g_re[b], sig, tmp1], writes=[init_im[g]])
                U = uu[b]
                c.op("pool", lambda e: e.tensor_tensor(out=U[0][:, :], in0=g_re[b][:, :], in1=crg[:, 0:512], op=ALU.mult), reads=[g_re[b], crg], writes=[U[0]])
                c.op("pool", lambda e: e.tensor_tensor(out=U[1][:, :], in0=g_im[b][:, :], in1=sig[:, 0:512], op=ALU.mult), reads=[g_im[b], sig], writes=[U[1]])
                c.op("pool", lambda e: e.tensor_tensor(out=U[2][:, :], in0=g_re[b][:, :], in1=sig[:, 0:512], op=ALU.mult), reads=[g_re[b], sig], writes=[U[2]])
                c.op("pool", lambda e: e.tensor_tensor(out=U[3][:, :], in0=g_im[b][:, :], in1=crg[:, 0:512], op=ALU.mult), reads=[g_im[b], crg], writes=[U[3]])
                c.op("pool", lambda e: e.tensor_tensor(out=h_re[b][:, :], in0=U[0][:, :], in1=U[1][:, :], op=ALU.subtract), reads=[U[0], U[1]], writes=[h_re[b]])
                c.op("pool", lambda e: e.tensor_tensor(out=h_im[b][:, :], in0=U[2][:, :], in1=U[3][:, :], op=ALU.add), reads=[U[2], U[3]], writes=[h_im[b]])
                yp = yps[ny % 2]
                c.op("pe", lambda e: e.matmul(yp[:, :], lhsT=CT_re[:, gsl], rhs=h_re[b][:, :], start=(g4 == 0), stop=False),
                     reads=[CT_re, h_re[b]], writes=[yp])
                c.op("pe", lambda e: e.matmul(yp[:, :], lhsT=CT_imn[:, gsl], rhs=h_im[b][:, :], start=False, stop=(g4 == 3)),
                     reads=[CT_imn, h_im[b]], writes=[yp])
                if g4 == 3:
                    yb, gb = ysb[ny % 2], gy[ny % 2]
                    ny += 1
                    c.op("dve", lambda e: e.scalar_tensor_tensor(out=yb[:, :], in0=uT[:, tt, csl], scalar=dcol[:, tt:tt + 1], in1=yp[:, :],
                                                                 op0=ALU.mult, op1=ALU.add), reads=[uT, dcol, yp], writes=[yb])
                    c.op("act", lambda e: e.activation(out=gb[:, :], in_=yb[:, :], func=AF.Gelu_apprx_tanh), reads=[yb], writes=[gb])
                    c.dma("sp", gyT_d[tt * 128:(tt + 1) * 128, csl], gb[:, :], reads=[gb])
        c.wait_all("sp", gy)
        c.barrier()


T3 = 2048
EPS3 = 1e-6


def phase3(c, oaT_d, szaT_d, gyT_d, szsT_d, x_d, wglu_d, wout_d, gpost_d, xo_d):
    nc = c.nc
    with ExitStack() as sc:
        oT = c.sb("c_oT", [128, 16, T3], BF16, scope=sc)
        oT_k = [c.view(f"c_oT{k}") for k in range(16)]
        gpost = c.sb("c_gpost", [128, 2048], F32, dsem=c.sem("c_gps"), scope=sc)
        c.dma("sp", gpost[:, :], gpost_d, writes=[gpost])
        with ExitStack() as s1:
            sa = [c.sb(f"c_sa{i}", [128, T3], BF16, dsem=c.sem(f"c_sas{i}"), scope=s1) for i in range(2)]
            sb_ = [c.sb(f"c_sb{i}", [128, T3], BF16, dsem=c.sem(f"c_sbs{i}"), scope=s1) for i in range(2)]
            wg = c.sb("c_wg", [128, 8, 2048], BF16, dsem=c.sem("c_wgs"), scope=s1)
            gyT = c.sb("c_gyT", [128, 8, T3], BF16, dsem=c.sem("c_gys"), scope=s1)
            wsrc = wglu_d.rearrange("(k p) e -> p k e", p=128)
            for q4 in range(4):
                c.dma("pool", wg[:, :, q4 * 512:(q4 + 1) * 512], wsrc[:, :, q4 * 512:(q4 + 1) * 512], writes=[wg])
            gsrc = gyT_d.rearrange("(k p) t -> p k t", p=128)
            for hf in range(2):
                c.dma("sp", gyT[:, hf * 4:(hf + 1) * 4, :], gsrc[:, hf * 4:(hf + 1) * 4, :], writes=[gyT])
            for kc in range(8):
                a, b = sa[kc % 2], sb_[kc % 2]
                c.dma("sp", a[:, :], oaT_d[kc * 128:(kc + 1) * 128, :], writes=[a])
                c.dma("sp", b[:, :], szaT_d[kc * 128:(kc + 1) * 128, :], writes=[b])
                c.op("pool", lambda e: e.tensor_tensor(out=oT[:, kc, :], in0=a[:, :], in1=b[:, :], op=ALU.mult),
                     reads=[a, b], writes=[oT_k[kc]])
            accA = [c.ps(f"c_accA{i}", [128, 512], F32, scope=s1) for i in range(2)]
            accB = [c.ps(f"c_accB{i}", [128, 512], F32, scope=s1) for i in range(2)]
            sg = [c.sb(f"c_sg{i}", [128, 512], F32, scope=s1) for i in range(2)]
            tv = [c.sb(f"c_tv{i}", [128, 512], F32, scope=s1) for i in range(2)]
            n = 0
            for j in range(8):
                zs = sa[j % 2]
                c.dma("sp", zs[:, :], szsT_d[j * 128:(j + 1) * 128, :], writes=[zs])
                for tg in range(4):
                    A, B = accA[n % 2], accB[n % 2]
                    sgb, tvb = sg[n % 2], tv[n % 2]
                    n += 1
                    tsl = slice(tg * 512, (tg + 1) * 512)
                    for kc in range(8):
                        c.op("pe", lambda e: e.matmul(A[:, :], lhsT=wg[:, kc, j * 128:(j + 1) * 128], rhs=gyT[:, kc, tsl],
                                                      start=(kc == 0), stop=(kc == 7)), reads=[wg, gyT], writes=[A])
                    for kc in range(8):
                        c.op("pe", lambda e: e.matmul(B[:, :], lhsT=wg[:, kc, 1024 + j * 128:1024 + (j + 1) * 128], rhs=gyT[:, kc, tsl],
                                                      start=(kc == 0), stop=(kc == 7)), reads=[wg, gyT], writes=[B])
                    c.op("act", lambda e: e.activation(out=sgb[:, :], in_=B[:, :], func=AF.Sigmoid), reads=[B], writes=[sgb])
                    c.op("dve", lambda e: e.tensor_tensor(out=tvb[:, :], in0=A[:, :], in1=sgb[:, :], op=ALU.mult),
                         reads=[A, sgb], writes=[tvb])
                    c.op("pool", lambda e: e.tensor_tensor(out=oT[:, 8 + j, tsl], in0=tvb[:, :], in1=zs[:, tsl], op=ALU.mult),
                         reads=[tvb, zs], writes=[oT_k[8 + j]])
            c.barrier()
        with ExitStack() as s2:
            wo = c.sb("c_wo", [128, 16, 2048], BF16, dsem=c.sem("c_wos"), scope=s2)
            wsrc = wout_d.rearrange("(k p) e -> p k e", p=128)
            for q4 in range(4):
                for hf in range(2):
                    c.dma("pool", wo[:, hf * 8:(hf + 1) * 8, q4 * 512:(q4 + 1) * 512],
                          wsrc[:, hf * 8:(hf + 1) * 8, q4 * 512:(q4 + 1) * 512], writes=[wo])
            acc = [[c.ps(f"c_acc{i}_{d}", [128, 512], F32, scope=s2) for d in range(4)] for i in range(2)]
            xt = [c.sb(f"c_xt{i}", [128, 2048], F32, dsem=c.sem(f"c_xts{i}"), scope=s2) for i in range(2)]
            xo = [c.sb(f"c_xo{i}", [128, 2048], F32, dsem=c.sem(f"c_xos{i}"), scope=s2) for i in range(2)]
            junk = c.sb("c_junk", [128, 512], BF16, scope=s2)
            ss4 = [c.sb(f"c_ss4{i}", [128, 4], F32, scope=s2) for i in range(2)]
            rstd = [c.sb(f"c_rstd{i}", [128, 1], F32, scope=s2) for i in range(2)]
            tmp = [c.sb(f"c_tmp{i}", [128, 512], F32, scope=s2) for i in range(2)]
            nt = 0
            for i in range(16):
                ac_, xb, xob, s4, r = acc[i % 2], xt[i % 2], xo[i % 2], ss4[i % 2], rstd[i % 2]
                c.dma("sp", xb[:, :], x_d[i * 128:(i + 1) * 128, :], writes=[xb])
                for d in range(4):
                    for kc in range(16):
                        c.op("pe", lambda e: e.matmul(ac_[d][:, :], lhsT=oT[:, kc, i * 128:(i + 1) * 128],
                                                      rhs=wo[:, kc, d * 512:(d + 1) * 512], start=(kc == 0), stop=(kc == 15)),
                             reads=[oT_k[kc], wo], writes=[ac_[d]])
                    c.op("act", lambda e: e.activation(out=junk[:, :], in_=ac_[d][:, :], func=AF.Square, accum_out=s4[:, d:d + 1]),
                         reads=[ac_[d]], writes=[junk, s4])
                c.op("dve", lambda e: e.reduce_sum(out=r[:, :], in_=s4[:, :], axis=mybir.AxisListType.X), reads=[s4], writes=[r])
                c.op("dve", lambda e: e.tensor_scalar(out=r[:, :], in0=r[:, :], scalar1=1.0 / 2048, scalar2=EPS3,
                                                      op0=ALU.mult, op1=ALU.add), reads=[r], writes=[r])
                c.op("act", lambda e: e.activation(out=r[:, :], in_=r[:, :], func=AF.Sqrt), reads=[r], writes=[r])
                c.op("dve", lambda e: e.reciprocal(out=r[:, :], in_=r[:, :]), reads=[r], writes=[r])
                for d in range(4):
                    tb = tmp[nt % 2]; nt += 1
                    dsl = slice(d * 512, (d + 1) * 512)
                    c.op("dve", lambda e: e.scalar_tensor_tensor(out=tb[:, :], in0=ac_[d][:, :], scalar=r[:, 0:1], in1=gpost[:, dsl],
                                                                 op0=ALU.mult, op1=ALU.mult), reads=[ac_[d], r, gpost], writes=[tb])
                    c.op("pool", lambda e: e.tensor_tensor(out=xob[:, dsl], in0=tb[:, :], in1=xb[:, dsl], op=ALU.add),
                         reads=[tb, xb], writes=[xob])
                c.dma("sp", xo_d[i * 128:(i + 1) * 128, :], xob[:, :], reads=[xob])
            c.wait_all("sp", xo)
            c.barrier()


import math
import numpy as np
import ml_dtypes
BF = ml_dtypes.bfloat16

def t5_bucket_np(rel):
    rel = np.asarray(rel, np.int32)
    try:
        import jax
        import jax.numpy as jnp
        cpu = jax.devices("cpu")[0]
        with jax.default_device(cpu):
            r = jnp.asarray(rel, dtype=jnp.int32)
            n = jnp.abs(r)
            nf = jnp.maximum(n, 1).astype(jnp.float32)
            large = 8 + (jnp.log(nf / 8) / math.log(128 / 8) * (16 - 8)).astype(jnp.int32)
            large = jnp.minimum(large, 15)
            return np.asarray(jnp.where(r > 0, 16, 0) + jnp.where(n < 8, n, large))
    except Exception:
        n = np.abs(rel)
        nf = np.maximum(n, 1).astype(np.float32)
        large = 8 + (np.log(nf / np.float32(8)) / np.float32(math.log(16)) * np.float32(8)).astype(np.int32)
        large = np.minimum(large, 15)
        return np.where(rel > 0, 16, 0) + np.where(n < 8, n, large)

_BK = None
def attn_consts(inp, l, j):
    global _BK
    if _BK is None:
        k = np.arange(128)[:, None]
        q = np.arange(128)[None, :]
        _BK = [t5_bucket_np(k - (q + 128 * dd)) for dd in range(2)]
    rb = inp["rel_bias"]
    biasD = np.zeros((128, 2, 2, 128), np.float32)
    c15 = np.zeros((128, 2), np.float32)
    for h in range(2):
        for dd in range(2):
            biasD[:, h, dd, :] = rb[_BK[dd], 2 * j + h]
        c15[:, h] = rb[15, 2 * j + h]
    maskD = np.zeros((128, 128), np.float32)
    maskD[64:, :64] = -30000.0
    lam4 = np.stack([inp["lambda_q1"][l], inp["lambda_k1"][l], inp["lambda_q2"][l], inp["lambda_k2"][l]], 0)
    lam4 = np.ascontiguousarray(np.broadcast_to(lam4[None], (128, 4, 64))).astype(np.float32)
    subg = np.ascontiguousarray(inp["subln_g"][l].reshape(128, 1)).astype(np.float32)
    ident = np.eye(128, dtype=np.float32).astype(BF)
    li = 0.8 - 0.6 * math.exp(-0.3 * l)
    liv = np.zeros((128, 2), np.float32); liv[:, 0] = li; liv[:, 1] = 1.0 - li
    return {"biasD": biasD, "maskD": maskD, "c15": c15, "lam4": lam4, "subg": subg, "ident": ident, "liv": liv}

def ssm_consts(inp, l, j):
    G0 = 16 * j
    a_re = inp["ssm_a_re"][l, G0:G0 + 16]
    a_im = inp["ssm_a_im"][l, G0:G0 + 16]
    ldt = inp["ssm_log_dt"][l, G0:G0 + 16]
    b_re = inp["ssm_b_re"][l, G0:G0 + 16]
    b_im = inp["ssm_b_im"][l, G0:G0 + 16]
    c_re = inp["ssm_c_re"][l, G0:G0 + 16]
    c_im = inp["ssm_c_im"][l, G0:G0 + 16]
    d = inp["ssm_d"][l, 256 * j:256 * (j + 1)]
    acol = np.zeros((128, 3, 8), np.float32)
    for gp in range(8):
        acol[:, 0, gp] = a_re[2 * gp:2 * gp + 2].reshape(128)
        acol[:, 1, gp] = a_im[2 * gp:2 * gp + 2].reshape(128)
        acol[:, 2, gp] = np.repeat(ldt[2 * gp:2 * gp + 2], 64)
    arow1 = np.stack([acol[:, k, :].T.reshape(1024) for k in range(3)], 0)
    arow = np.ascontiguousarray(np.broadcast_to(arow1[None], (128, 3, 1024))).astype(np.float32)
    BTr = np.zeros((128, 8, 128), np.float32); BTi = np.zeros((128, 8, 128), np.float32)
    CTr = np.zeros((128, 8, 128), np.float32); CTi = np.zeros((128, 8, 128), np.float32)
    for gp in range(8):
        g4 = gp % 4
        for g2 in range(2):
            g = 2 * gp + g2
            rows = slice(g4 * 32 + g2 * 16, g4 * 32 + g2 * 16 + 16)
            cols = slice(g2 * 64, g2 * 64 + 64)
            BTr[rows, gp, cols] = b_re[g].T
            BTi[rows, gp, cols] = b_im[g].T
            CTr[cols, gp, rows] = c_re[g].T
            CTi[cols, gp, rows] = c_im[g].T
    dcol = np.ascontiguousarray(d.reshape(2, 128).T).astype(np.float32)
    tau = np.ascontiguousarray(np.broadcast_to(np.arange(520, dtype=np.float32)[None], (128, 520)))
    return {"acol": acol, "arow": arow, "BTr": BTr, "BTi": BTi, "CTr": CTr, "CTi": CTi, "dcol": dcol, "tau": tau}


from concourse.bass_utils import run_bass_kernel_spmd

AC_SHAPES = {"biasD": [128, 2, 2, 128], "maskD": [128, 128], "c15": [128, 2], "lam4": [128, 4, 64],
             "subg": [128, 1], "liv": [128, 2]}
SC_SHAPES = {"acol": [128, 3, 8], "arow": [128, 3, 1024], "BTr": [128, 8, 128], "BTi": [128, 8, 128],
             "CTr": [128, 8, 128], "CTi": [128, 8, 128], "dcol": [128, 2], "tau": [128, 520]}
P1_OUTS = ["qT", "kT", "szaT", "uT", "szsT"]


def build_p1():
    nc = bass.Bass("TRN2", target_bir_lowering=False)
    x = nc.dram_tensor("x", [2048, 2048], F32, kind="ExternalInput").ap()
    w = nc.dram_tensor("w", [2048, 6144], F32, kind="ExternalInput").ap()
    g = nc.dram_tensor("g", [128, 16], F32, kind="ExternalInput").ap()
    idn = nc.dram_tensor("idn", [128, 128], BF16, kind="ExternalInput").ap()
    outs = {n: nc.dram_tensor(n, [1024, 2048], BF16, kind="ExternalOutput").ap() for n in P1_OUTS}
    outs["v"] = nc.dram_tensor("v", [2048, 1024], BF16, kind="ExternalOutput").ap()
    c = Ctx(nc)
    phase1(c, x, w, g, idn, outs)
    c.close()
    return nc


def build_p2():
    nc = bass.Bass("TRN2", target_bir_lowering=False)
    qT = nc.dram_tensor("qT", [2, 128, S], BF16, kind="ExternalInput").ap()
    kT = nc.dram_tensor("kT", [2, 128, S], BF16, kind="ExternalInput").ap()
    v = nc.dram_tensor("v", [S, 256], BF16, kind="ExternalInput").ap()
    uT = nc.dram_tensor("uT", [256, S], BF16, kind="ExternalInput").ap()
    ac = {k: nc.dram_tensor("ac_" + k, s, F32, kind="ExternalInput").ap() for k, s in AC_SHAPES.items()}
    ac["ident"] = nc.dram_tensor("ac_ident", [128, 128], BF16, kind="ExternalInput").ap()
    scn = {k: nc.dram_tensor("sc_" + k, s, F32, kind="ExternalInput").ap() for k, s in SC_SHAPES.items()}
    oaT = nc.dram_tensor("oaT", [256, S], BF16, kind="ExternalOutput").ap()
    gyT = nc.dram_tensor("gyT", [256, S], BF16, kind="ExternalOutput").ap()
    c = Ctx(nc)
    phase2_ssm(c, uT, scn, gyT)
    phase2_attn(c, qT, kT, v, ac, oaT)
    c.close()
    return nc


def build_p3():
    nc = bass.Bass("TRN2", target_bir_lowering=False)
    ins = {n: nc.dram_tensor(n, [1024, 2048], BF16, kind="ExternalInput").ap() for n in ["oaT", "szaT", "gyT", "szsT"]}
    x = nc.dram_tensor("x", [2048, 2048], F32, kind="ExternalInput").ap()
    wg = nc.dram_tensor("wg", [1024, 2048], F32, kind="ExternalInput").ap()
    wo = nc.dram_tensor("wo", [2048, 2048], F32, kind="ExternalInput").ap()
    gp = nc.dram_tensor("gp", [128, 2048], F32, kind="ExternalInput").ap()
    xo = nc.dram_tensor("xo", [2048, 2048], F32, kind="ExternalOutput").ap()
    c = Ctx(nc)
    phase3(c, ins["oaT"], ins["szaT"], ins["gyT"], ins["szsT"], x, wg, wo, gp, xo)
    c.close()
    return nc


def kernel(**inp):
    inp = {k: np.asarray(v) for k, v in inp.items()}
    cores = list(range(8))
    x = np.ascontiguousarray(inp["x"], dtype=np.float32)
    xs = [np.ascontiguousarray(x[r // 4, (r % 4) * 2048:(r % 4 + 1) * 2048]) for r in cores]
    idn = np.eye(128, dtype=np.float32).astype(BF)
    nc1, nc2, nc3 = build_p1(), build_p2(), build_p3()
    for l in range(4):
        w = np.ascontiguousarray(inp["w_in"][l], dtype=np.float32)
        gT = np.ascontiguousarray(inp["pre_norm_g"][l].reshape(16, 128).T, dtype=np.float32)
        r1 = run_bass_kernel_spmd(nc1, [{"x": xs[r], "w": w, "g": gT, "idn": idn} for r in cores], core_ids=cores).results
        in2 = []
        for r in cores:
            b, j = r // 4, r % 4
            src = [r1[b * 4 + i] for i in range(4)]
            qT2 = np.stack([np.concatenate([np.asarray(s_["qT"])[(2 * j + h) * 128:(2 * j + h + 1) * 128] for s_ in src], 1)
                            for h in range(2)], 0)
            kT2 = np.stack([np.concatenate([np.asarray(s_["kT"])[(2 * j + h) * 128:(2 * j + h + 1) * 128] for s_ in src], 1)
                            for h in range(2)], 0)
            v2 = np.concatenate([np.asarray(s_["v"])[:, 256 * j:256 * (j + 1)] for s_ in src], 0)
            u2 = np.concatenate([np.asarray(s_["uT"])[256 * j:256 * (j + 1)] for s_ in src], 1)
            m = {"qT": np.ascontiguousarray(qT2), "kT": np.ascontiguousarray(kT2), "v": np.ascontiguousarray(v2),
                 "uT": np.ascontiguousarray(u2)}
            m.update({"ac_" + k: v_ for k, v_ in attn_consts(inp, l, j).items()})
            m.update({"sc_" + k: v_ for k, v_ in ssm_consts(inp, l, j).items()})
            in2.append(m)
        r2 = run_bass_kernel_spmd(nc2, in2, core_ids=cores).results
        wg = np.ascontiguousarray(inp["w_glu"][l], dtype=np.float32)
        wo = np.ascontiguousarray(inp["w_out"][l], dtype=np.float32)
        gp = np.ascontiguousarray(np.broadcast_to(inp["post_norm_g"][l][None], (128, 2048)), dtype=np.float32)
        in3 = []
        for r in cores:
            b, i = r // 4, r % 4
            src = [r2[b * 4 + j] for j in range(4)]
            oaT = np.concatenate([np.asarray(s_["oaT"])[:, i * 2048:(i + 1) * 2048] for s_ in src], 0)
            gyT = np.concatenate([np.asarray(s_["gyT"])[:, i * 2048:(i + 1) * 2048] for s_ in src], 0)
            in3.append({"oaT": np.ascontiguousarray(oaT), "gyT": np.ascontiguousarray(gyT),
                        "szaT": np.asarray(r1[r]["szaT"]), "szsT": np.asarray(r1[r]["szsT"]),
                        "x": xs[r], "wg": wg, "wo": wo, "gp": gp})
        r3 = run_bass_kernel_spmd(nc3, in3, core_ids=cores).results
        xs = [np.ascontiguousarray(np.asarray(r3[r]["xo"]), dtype=np.float32) for r in cores]
    out = np.zeros((2, 8192, 2048), np.float32)
    for r in cores:
        out[r // 4, (r % 4) * 2048:(r % 4 + 1) * 2048] = xs[r]
    return out
```
